# Optimizing a Trainium2 kernel written in Bass

```python
import math
import jax, jax.numpy as jnp
from jax import lax
import numpy as np

D_MODEL = 2048
BATCH = 2
SEQ = 8192
DEPTH = 4

N_MEM = 256
D_MIX = 2 * D_MODEL
D_SSD = D_MIX // 2
D_SC = D_MIX - D_SSD
SSD_HEADDIM = 64
SSD_HEADS = D_SSD // SSD_HEADDIM
SSD_GROUPS = 4
SSD_STATE = 128
SSD_CONV = 4
SSD_CHUNK = 256
D_XBC = D_SSD + 2 * SSD_GROUPS * SSD_STATE
DT_MIN = 1e-3
DT_MAX = 1e-1
SC_CONV = 3
SC_GROUPS = 16
XA_HEADS = 4
XA_HEADDIM = 128
D_XA = XA_HEADS * XA_HEADDIM
D_FF = ((8 * D_MODEL + 3 * 256 - 1) // (3 * 256)) * 256
NORM_EPS = 1e-5
D_IN_PROJ = D_SSD + D_XBC + SSD_HEADS + 3 * D_SC
IN_SPLITS = [D_SSD,
             D_SSD + D_XBC,
             D_SSD + D_XBC + SSD_HEADS,
             D_SSD + D_XBC + SSD_HEADS + D_SC,
             D_SSD + D_XBC + SSD_HEADS + 2 * D_SC]

kernel_name = "hybrid_ssd_shortconv_memxattn_trunk"


def rmsnorm(x, g):
    xf = x.astype(jnp.float32)
    xf = xf * lax.rsqrt(jnp.mean(xf * xf, axis=-1, keepdims=True) + NORM_EPS)
    return (xf * g.astype(jnp.float32)).astype(x.dtype)


def grouped_rmsnorm(x, g, n_groups):
    shp = x.shape
    xf = x.astype(jnp.float32).reshape(shp[:-1] + (n_groups, shp[-1] // n_groups))
    xf = xf * lax.rsqrt(jnp.mean(xf * xf, axis=-1, keepdims=True) + NORM_EPS)
    return (xf.reshape(shp) * g.astype(jnp.float32)).astype(x.dtype)


def causal_dwconv(x, w):
    k, c = w.shape
    return lax.conv_general_dilated(
        x, w[:, None, :].astype(x.dtype), window_strides=(1,), padding=[(k - 1, 0)],
        dimension_numbers=("NWC", "WIO", "NWC"), feature_group_count=c)


def ssd_chunked(xh, dt, a, bm, cm):
    f32 = jnp.float32
    bsz, seqlen, nh, hd = xh.shape
    g, n = bm.shape[2], bm.shape[3]
    r = nh // g
    pad = (-seqlen) % SSD_CHUNK
    xh, dt, bm, cm = (t.astype(f32) for t in (xh, dt, bm, cm))
    if pad:
        xh = jnp.pad(xh, ((0, 0), (0, pad), (0, 0), (0, 0)))
        dt = jnp.pad(dt, ((0, 0), (0, pad), (0, 0)))
        bm = jnp.pad(bm, ((0, 0), (0, pad), (0, 0), (0, 0)))
        cm = jnp.pad(cm, ((0, 0), (0, pad), (0, 0), (0, 0)))
    nc = (seqlen + pad) // SSD_CHUNK
    L = SSD_CHUNK
    x = (xh * dt[..., None]).reshape(bsz, nc, L, g, r, hd)
    a_dt = jnp.moveaxis((dt * a.astype(f32)).reshape(bsz, nc, L, g, r), 2, -1)
    a_cs = jnp.cumsum(a_dt, axis=-1)
    bc = bm.reshape(bsz, nc, L, g, n)
    cc = cm.reshape(bsz, nc, L, g, n)
    causal = jnp.tril(jnp.ones((L, L), dtype=bool))
    decay = jnp.exp(jnp.where(causal, a_cs[..., :, None] - a_cs[..., None, :], -jnp.inf))
    cb = jnp.einsum("bclgn,bcsgn->bcgls", cc, bc)
    scores = cb[:, :, :, None] * decay
    y_diag = jnp.einsum("bcgrls,bcsgrp->bclgrp", scores, x)
    decay_states = jnp.exp(a_cs[..., -1:] - a_cs)
    states = jnp.einsum("bclgn,bcgrl,bclgrp->bcgrpn", bc, decay_states, x)
    chunk_decay = jnp.exp(a_cs[..., -1])

    def step(h, inp):
        s_c, d_c = inp
        return h * d_c[..., None, None] + s_c, h

    h0 = jnp.zeros_like(states[:, 0])
    _, prev = lax.scan(step, h0, (jnp.moveaxis(states, 1, 0), jnp.moveaxis(chunk_decay, 1, 0)))
    prev = jnp.moveaxis(prev, 0, 1)
    y_off = jnp.einsum("bclgn,bcgrpn,bcgrl->bclgrp", cc, prev, jnp.exp(a_cs))
    y = (y_diag + y_off).reshape(bsz, nc * L, nh, hd)
    return y[:, :seqlen]


def hybrid_mixer(h, w_in, ssd_conv_w, ssd_conv_b, dt_bias, a_log, d_skip, ssd_norm,
                 sc_conv_w, sc_norm, w_out):
    bsz, seqlen, _ = h.shape
    proj = h @ w_in
    z, xbc, dt_raw, sc_u, sc_b, sc_c = jnp.split(proj, IN_SPLITS, axis=-1)
    xbc = jax.nn.silu(causal_dwconv(xbc, ssd_conv_w) + ssd_conv_b)
    xs, bm, cm = jnp.split(xbc, [D_SSD, D_SSD + SSD_GROUPS * SSD_STATE], axis=-1)
    xh = xs.reshape(bsz, seqlen, SSD_HEADS, SSD_HEADDIM)
    dt = jax.nn.softplus(dt_raw.astype(jnp.float32) + dt_bias.astype(jnp.float32))
    a = -jnp.exp(a_log.astype(jnp.float32))
    y = ssd_chunked(xh, dt, a,
                    bm.reshape(bsz, seqlen, SSD_GROUPS, SSD_STATE),
                    cm.reshape(bsz, seqlen, SSD_GROUPS, SSD_STATE))
    y = y + xh.astype(jnp.float32) * d_skip.astype(jnp.float32)[:, None]
    y = y.reshape(bsz, seqlen, D_SSD).astype(h.dtype)
    y_ssd = grouped_rmsnorm(y * jax.nn.silu(z), ssd_norm, SSD_GROUPS)
    v = sc_b * causal_dwconv(sc_c * sc_u, sc_conv_w)
    y_sc = grouped_rmsnorm(v, sc_norm, SC_GROUPS)
    return jnp.concatenate([y_ssd, y_sc], axis=-1) @ w_out


def memory_cross_attention(h, mem_n, w_q, w_k, w_v, w_o):
    bsz, seqlen, _ = h.shape
    n_mem = mem_n.shape[1]
    q = (h @ w_q).reshape(bsz, seqlen, XA_HEADS, XA_HEADDIM)
    k = (mem_n @ w_k).reshape(bsz, n_mem, XA_HEADS, XA_HEADDIM)
    v = (mem_n @ w_v).reshape(bsz, n_mem, XA_HEADS, XA_HEADDIM)
    scores = jnp.einsum("bshd,bmhd->bhsm", q, k).astype(jnp.float32) * (XA_HEADDIM ** -0.5)
    p = jax.nn.softmax(scores, axis=-1).astype(v.dtype)
    o = jnp.einsum("bhsm,bmhd->bshd", p, v).reshape(bsz, seqlen, D_XA)
    return o @ w_o


def swiglu(h, w_gate, w_up, w_down):
    return (jax.nn.silu(h @ w_gate) * (h @ w_up)) @ w_down


def setup_inputs(seed: int = 0) -> dict:
    key = jax.random.key(seed)
    ks = jax.random.split(key, 24)
    f32 = jnp.float32

    def nrm(k, shape, scale):
        return jax.random.normal(k, shape, f32) * scale

    def gain(k, shape):
        return 1.0 + 0.02 * jax.random.normal(k, shape, f32)

    x = nrm(ks[0], (BATCH, SEQ, D_MODEL), 1.0)
    mem = nrm(ks[1], (BATCH, N_MEM, D_MODEL), 1.0)
    norm_mix = gain(ks[2], (DEPTH, D_MODEL))
    w_in = nrm(ks[3], (DEPTH, D_MODEL, D_IN_PROJ), D_MODEL ** -0.5)
    ssd_conv_w = nrm(ks[4], (DEPTH, SSD_CONV, D_XBC), SSD_CONV ** -0.5)
    ssd_conv_b = nrm(ks[5], (DEPTH, D_XBC), 0.02)
    dt0 = jnp.exp(jax.random.uniform(ks[6], (DEPTH, SSD_HEADS), f32, math.log(DT_MIN), math.log(DT_MAX)))
    dt_bias = dt0 + jnp.log(-jnp.expm1(-dt0))
    a_log = jnp.log(jax.random.uniform(ks[7], (DEPTH, SSD_HEADS), f32, 1.0, 16.0))
    d_skip = gain(ks[8], (DEPTH, SSD_HEADS))
    ssd_norm = gain(ks[9], (DEPTH, D_SSD))
    sc_conv_w = nrm(ks[10], (DEPTH, SC_CONV, D_SC), SC_CONV ** -0.5)
    sc_norm = gain(ks[11], (DEPTH, D_SC))
    w_out = nrm(ks[12], (DEPTH, D_MIX, D_MODEL), D_MIX ** -0.5)
    mem_norm = gain(ks[13], (D_MODEL,))
    norm_xa = gain(ks[14], (DEPTH, D_MODEL))
    w_q = nrm(ks[15], (DEPTH, D_MODEL, D_XA), D_MODEL ** -0.5)
    w_k = nrm(ks[16], (DEPTH, D_MODEL, D_XA), D_MODEL ** -0.5)
    w_v = nrm(ks[17], (DEPTH, D_MODEL, D_XA), D_MODEL ** -0.5)
    w_o = nrm(ks[18], (DEPTH, D_XA, D_MODEL), D_XA ** -0.5)
    norm_ffn = gain(ks[19], (DEPTH, D_MODEL))
    w_gate = nrm(ks[20], (DEPTH, D_MODEL, D_FF), D_MODEL ** -0.5)
    w_up = nrm(ks[21], (DEPTH, D_MODEL, D_FF), D_MODEL ** -0.5)
    w_down = nrm(ks[22], (DEPTH, D_FF, D_MODEL), D_FF ** -0.5)
    norm_final = gain(ks[23], (D_MODEL,))
    return {"x": x, "mem": mem, "norm_mix": norm_mix, "w_in": w_in,
            "ssd_conv_w": ssd_conv_w, "ssd_conv_b": ssd_conv_b, "dt_bias": dt_bias,
            "a_log": a_log, "d_skip": d_skip, "ssd_norm": ssd_norm,
            "sc_conv_w": sc_conv_w, "sc_norm": sc_norm, "w_out": w_out,
            "mem_norm": mem_norm, "norm_xa": norm_xa, "w_q": w_q, "w_k": w_k,
            "w_v": w_v, "w_o": w_o, "norm_ffn": norm_ffn, "w_gate": w_gate,
            "w_up": w_up, "w_down": w_down, "norm_final": norm_final}


def reference(x, mem, norm_mix, w_in, ssd_conv_w, ssd_conv_b, dt_bias, a_log, d_skip,
              ssd_norm, sc_conv_w, sc_norm, w_out, mem_norm, norm_xa, w_q, w_k, w_v, w_o,
              norm_ffn, w_gate, w_up, w_down, norm_final):
    mem_n = rmsnorm(mem, mem_norm)
    h = x
    for i in range(DEPTH):
        h = h + hybrid_mixer(rmsnorm(h, norm_mix[i]), w_in[i], ssd_conv_w[i], ssd_conv_b[i],
                             dt_bias[i], a_log[i], d_skip[i], ssd_norm[i],
                             sc_conv_w[i], sc_norm[i], w_out[i])
        h = h + memory_cross_attention(rmsnorm(h, norm_xa[i]), mem_n, w_q[i], w_k[i], w_v[i], w_o[i])
        h = h + swiglu(rmsnorm(h, norm_ffn[i]), w_gate[i], w_up[i], w_down[i])
    return rmsnorm(h, norm_final)
```

```python
import numpy as np
import concourse.bass as bass
import concourse.mybir as mybir
from concourse.bass_utils import run_bass_kernel_spmd

F32 = mybir.dt.float32
BF16 = mybir.dt.bfloat16
AF = mybir.ActivationFunctionType
ALU = mybir.AluOpType

T = 2048
D = 2048
KC = 16
TT = 512
NTT = 4
DEPTH = 4
D_IN = 11296
D_FF = 5632
EPS = 1e-5
NCH = 16
OFF_Z, OFF_XBC, OFF_DT, OFF_U, OFF_B, OFF_C = 0, 2048, 5120, 5152, 7200, 9248


class Buf:
    __slots__ = ("name", "w", "r", "excl")

    def __init__(self, name, excl=False):
        self.name = name
        self.w = None
        self.r = []
        self.excl = excl


class Rec:
    ENG = ("pe", "act", "dve", "pool", "sp")

    def __init__(self, nc, sems):
        self.nc = nc
        self.items = {e: [] for e in self.ENG}
        self.seen = {e: {} for e in self.ENG}
        self.cnt = {e: 0 for e in self.ENG}
        self.sem = {}
        it = iter(sems)
        for e in ("pe", "act", "dve", "pool"):
            self.sem[e] = next(it)
        self.dsem = {"sp": [next(it) for _ in range(24)], "pool": [next(it) for _ in range(10)]}
        self.dcnt = {}
        self.drr = {"sp": 0, "pool": 0}
        self.semobj = {}
        self.limit = None
        self.log = []
        self.total = 0
        self.marks = []
        for e in ("pe", "act", "dve", "pool"):
            self.semobj[("c", e)] = self.sem[e]
        for q in self.dsem:
            for i, s in enumerate(self.dsem[q]):
                self.semobj[("d", q, i)] = s
                self.dcnt[("d", q, i)] = 0

    def _deps(self, reads, writes):
        deps = set()
        for b in reads:
            if b.w is not None:
                deps.add(b.w)
            if b.excl:
                deps.update(b.r)
        for b in writes:
            if b.w is not None:
                deps.add(b.w)
            deps.update(b.r)
        return deps

    def _waits(self, eng, deps):
        seen = self.seen[eng]
        best = {}
        for (k, v) in deps:
            if eng == "pe" and k == ("c", "pe"):
                continue
            if seen.get(k, 0) >= v:
                continue
            if best.get(k, 0) < v:
                best[k] = v
        out = []
        for k, v in best.items():
            seen[k] = v
            out.append((self.semobj[k], v))
        return out

    def _mark(self, tk, reads, writes):
        for b in reads:
            b.r.append(tk)
        for b in writes:
            b.w = tk
            b.r = []

    def mark(self, name):
        self.marks.append((name, self.total))

    def _over(self):
        self.total += 1
        return self.limit is not None and self.total > self.limit

    def op(self, eng, fn, reads=(), writes=()):
        if self._over():
            return None
        deps = self._deps(reads, writes)
        waits = self._waits(eng, deps)
        self.cnt[eng] += 1
        k = ("c", eng)
        tk = (k, self.cnt[eng])
        self.items[eng].append((waits, fn, (self.sem[eng], 1)))
        self.log.append((self.total, eng, [(k, v) for (k, v) in deps], tk, [b.name for b in reads], [b.name for b in writes]))
        self._mark(tk, reads, writes)
        return tk

    def dma(self, q, out, in_, reads=(), writes=()):
        if self._over():
            return None
        i = self.drr[q]
        self.drr[q] = (i + 1) % len(self.dsem[q])
        k = ("d", q, i)
        deps = self._deps(reads, writes)
        if self.dcnt[k] > 0:
            deps.add((k, self.dcnt[k]))
        waits = self._waits(q, deps)
        self.dcnt[k] += 16
        tk = (k, self.dcnt[k])
        self.items[q].append((waits, lambda e, out=out, in_=in_: e.dma_start(out=out, in_=in_), (self.semobj[k], 16)))
        self.log.append((self.total, "dma-" + q, [(kk, v) for (kk, v) in deps], tk, [b.name for b in reads], [b.name for b in writes]))
        self._mark(tk, reads, writes)
        return tk

    def coll_sem(self, semobj, key, ins, outs, groups, reads=(), writes=()):
        self.total += 1
        deps = self._deps(reads, writes)
        waits = self._waits("pool", deps)
        self.semobj[key] = semobj
        self.dcnt[key] = self.dcnt.get(key, 0) + 1
        tk = (key, self.dcnt[key])
        self.items["pool"].append((waits, lambda e: e.collective_compute(
            "AllGather", ALU.bypass, replica_groups=groups, ins=[a.opt() for a in ins], outs=[a.opt() for a in outs]), (semobj, 1)))
        self._mark(tk, reads, writes)
        return tk

    def coll(self, ins, outs, groups, reads=(), writes=()):
        return self.op("pool", lambda e: e.collective_compute("AllGather", ALU.bypass, replica_groups=groups,
                                                              ins=[a.opt() for a in ins], outs=[a.opt() for a in outs]), reads, writes)

    def barrier(self):
        allt = []
        for e in ("pe", "act", "dve", "pool"):
            if self.cnt[e] > 0:
                allt.append((("c", e), self.cnt[e]))
        for k, v in self.dcnt.items():
            if v > 0 and k[0] != "w":
                allt.append((k, v))
        for e in self.ENG:
            deps = [t for t in allt if not (t[0] == ("c", e))]
            waits = self._waits(e, deps)
            if waits:
                self.items[e].append((waits, None, None))

    def emit(self, block):
        names = {"pe": "tensor", "act": "scalar", "dve": "vector", "pool": "gpsimd", "sp": "sync"}
        for e in self.ENG:
            items = self.items[e]

            def body(eng, items=items):
                for waits, fn, inc in items:
                    for (s, v) in waits:
                        eng.wait_ge(s, v)
                    if fn is None:
                        continue
                    ins = fn(eng)
                    if inc is not None:
                        ins.then_inc(inc[0], inc[1])

            getattr(block, names[e])(body)


class Arena:
    def __init__(self, ap, nwords):
        self.ap = ap
        self.n = nwords
        self.off = 0

    def reset(self, off=0):
        self.off = off

    def f32(self, n):
        n0 = n
        n = (n + 7) // 8 * 8
        assert self.off + n <= self.n, ("arena overflow", self.off, n, self.n)
        a = self.ap[:, self.off:self.off + n0]
        self.off += n
        return a

    def bf16(self, n):
        w = (n + 1) // 2
        w = (w + 7) // 8 * 8
        a = self.f32(w).bitcast(BF16)
        return a[:, 0:n]


def weights_needed(stop):
    allw = ["w_in", "w_out", "w_q", "w_k", "w_v", "w_o", "w_gate", "w_up", "w_down"]
    if stop in ("A", "B", "X", "C"):
        return {"w_in"}
    if stop == "D":
        return {"w_in", "w_out"}
    if stop == "E":
        return {"w_in", "w_out", "w_q", "w_k", "w_v", "w_o"}
    return set(allw)


class _Stop(Exception):
    pass


def build_program(nlayers=DEPTH, dbg=False, stop=None, unit=None, limit=None):
    nc = bass.Bass("TRN2", target_bir_lowering=False)
    NLW = nlayers
    need = set() if unit else weights_needed(stop)
    _din0 = lambda name, shape: nc.dram_tensor(name, list(shape), F32, kind="ExternalInput").ap()
    din = lambda name, shape: _din0(name, shape if (not name.startswith("w_") or name in need) else [128, shape[1]])
    xT_d = din("xT", [D, T])
    xhalo_d = din("xhalo", [128, 48])
    memT_d = din("memT", [D, 256])
    WSPEC = [("w_in", D, D_IN), ("w_out", 2 * D, D), ("w_q", D, 512), ("w_k", D, 512), ("w_v", D, 512),
             ("w_o", 512, D), ("w_gate", D, D_FF), ("w_up", D, D_FF), ("w_down", D_FF, D)]
    wext, wbnc, wfull = {}, {}, {}
    for (nm, rows, cols) in WSPEC:
        if unit:
            wext[nm] = _din0(nm, [128, cols])
            wfull[nm] = [wext[nm]] * nlayers
        else:
            wext[nm] = _din0(nm, [NLW * rows // 8, cols])
            wbnc[nm] = nc.dram_tensor(nm + "_bnc", [NLW * rows // 8, cols], F32).ap()
            _wf2 = [nc.dram_tensor(f"{nm}_full{l_}", [rows, cols], F32).ap() for l_ in range(2)]
            wfull[nm] = [_wf2[l_ % 2] for l_ in range(nlayers)]
    NP128 = DEPTH * (16 * 3 + 4 * 24 + 24 + 3 * 16 + 16 + 16 + 16) + 32
    p128_d = din("p128", [128, NP128])
    p32_d = din("p32", [32, 2 * DEPTH])
    c128_d = din("c128", [128, 128 * 3])
    sel32_d = din("sel32", [128, 32 * 128])
    sel2_d = din("sel2", [128, 16 * 128])
    c32_d = din("c32", [32, 32 + 128])
    core_d = din("corec", [128, 16])
    outT_d = nc.dram_tensor("outT", [D, T], F32, kind="ExternalOutput").ap()
    if dbg:
        dbg_d = nc.dram_tensor("dbg", [3 * D, T], F32, kind="ExternalOutput").ap()
    if unit == "B":
        u_xbc = _din0("u_xbc", [3072, T])
        u_zs = _din0("u_zs", [D, T])
        u_dt = _din0("u_dt", [32, T])
        u_yl = nc.dram_tensor("u_yl", [D, T], F32, kind="ExternalOutput").ap()
        u_ycat = nc.dram_tensor("u_ycat", [D, T], BF16, kind="ExternalOutput").ap()

    hT_d = nc.dram_tensor("hT_s", [D, T], F32).ap()
    zsT_d = nc.dram_tensor("zsT_s", [D, T], F32).ap()
    xbcT_d = nc.dram_tensor("xbcT_s", [3072, T], F32).ap()
    ylT_d = nc.dram_tensor("ylT_s", [D, T], F32).ap()
    ycatT_d = nc.dram_tensor("ycatT_s", [2 * D, T], BF16).ap()
    gS_in = nc.dram_tensor("gS_in", [128, 2048], F32).ap()
    gS_out = nc.dram_tensor("gS_out", [4 * 128, 2048], F32).ap()
    gD_in = nc.dram_tensor("gD_in", [1, 32], F32).ap()
    gD_out = nc.dram_tensor("gD_out", [4, 32], F32).ap()
    gH_in = nc.dram_tensor("gH_in", [128, 48], F32).ap()
    gH_out = nc.dram_tensor("gH_out", [4 * 128, 48], F32).ap()
    memn_d = nc.dram_tensor("memn_s", [128, 16 * 256], BF16).ap()
    GROUPS = [[0, 1, 2, 3], [4, 5, 6, 7]]

    def fm(ap):
        return ap.rearrange("(k p) t -> p k t", p=128)

    hT_v, zsT_v, xbcT_v, ylT_v, ycatT_v = fm(hT_d), fm(zsT_d), fm(xbcT_d), fm(ylT_d), fm(ycatT_d)
    xT_v, outT_v, memT_v = fm(xT_d), fm(outT_d), fm(memT_d)

    ARENA_W = 45120
    import contextlib
    with contextlib.ExitStack() as es:
        E = es.enter_context
        arena_t = E(nc.sbuf_tensor("arena", [128, ARENA_W], F32))
        p128 = E(nc.sbuf_tensor("p128s", [128, NP128], F32))
        p32 = E(nc.sbuf_tensor("p32s", [32, 2 * DEPTH + 2], F32))
        c128f = E(nc.sbuf_tensor("c128f", [128, 384], F32))
        c128b = E(nc.sbuf_tensor("c128b", [128, 384], BF16))
        c32 = E(nc.sbuf_tensor("c32s", [32, 160], F32))
        corec = E(nc.sbuf_tensor("corecs", [128, 16], F32))
        hhalo = E(nc.sbuf_tensor("hhalo", [128, 48], F32))
        dtT = E(nc.sbuf_tensor("dtT", [128, T], F32))
        AT = E(nc.sbuf_tensor("AT", [128, T], F32))
        decin = E(nc.sbuf_tensor("decin", [128, T], F32))
        epst = E(nc.sbuf_tensor("epst", [128, 4], F32))
        ps_t = E(nc.psum_tensor("ps", [128, 8, 512], F32))
        sems = [E(nc.semaphore(f"s{i}")) for i in range(38 + 18)]
        block = E(nc.Block())

        R = Rec(nc, sems)
        R.limit = limit
        nc._rec = R
        A = Arena(arena_t[:], ARENA_W)
        PB = [Buf(f"psum{i}", excl=True) for i in range(8)]
        ps = ps_t[:]

        def bank(i):
            return ps[:, i, :]

        identf, onesf, negmf = c128f[:, 0:128], c128f[:, 128:256], c128f[:, 256:384]
        identb, onesb, negmb = c128b[:, 0:128], c128b[:, 128:256], c128b[:, 256:384]
        ident32, ones32 = c32[:, 0:32], c32[:, 32:160]
        B_const = Buf("const")
        B_memn = Buf("memn")
        B_hhalo = Buf("hhalo")
        B_dtT, B_AT, B_decin, B_tokp = Buf("dtT"), Buf("AT"), Buf("decin"), Buf("tokp")
        B_hT = [[Buf(f"hT{k}_{t}") for t in range(NTT)] for k in range(KC)]
        B_zs = [Buf(f"zs{k}") for k in range(16)]
        B_xbc = [Buf(f"xbc{k}") for k in range(24)]
        B_yl = [Buf(f"yl{k}") for k in range(16)]
        B_ycat = [Buf(f"ycat{k}") for k in range(32)]
        B_gS_in, B_gS_out, B_gH_in, B_gH_out = Buf("gSi"), Buf("gSo"), Buf("gHi"), Buf("gHo")
        B_gD_in, B_gD_out = Buf("gDi"), Buf("gDo")
        B_out = Buf("out")

        pc = {}
        o = 0
        for nm, n in (("norm_mix", 16), ("norm_xa", 16), ("norm_ffn", 16), ("conv_w", 96), ("conv_b", 24),
                      ("sc_w", 48), ("ssd_norm", 16), ("sc_norm", 16), ("dfeat", 16)):
            pc[nm] = (o, n)
            o += DEPTH * n
        pc["norm_final"] = (o, 16)
        pc["mem_norm"] = (o + 16, 16)

        def pcol(nm, l, j):
            o, n = pc[nm]
            c = o + (l * n if nm not in ("norm_final", "mem_norm") else 0) + j
            return p128[:, c:c + 1]


        def mm(out, obuf, pairs, rbufs, start=True, stop=True):
            def fn(pe):
                n = len(pairs)
                last = None
                for i, (l, r) in enumerate(pairs):
                    last = pe.matmul(out, lhsT=l, rhs=r, start=(start and i == 0), stop=(stop and i == n - 1))
                return last
            return R.op("pe", fn, reads=rbufs, writes=[obuf])

        def tr(out, obuf, in_, ident, rbufs):
            return R.op("pe", lambda pe: pe.transpose(out, in_, ident), reads=rbufs, writes=[obuf])

        def act(out, in_, func, reads, writes, bias=None, scale=None):
            kw = {}
            if bias is not None:
                kw["bias"] = bias
            if scale is not None:
                kw["scale"] = scale
            return R.op("act", lambda e: e.activation(out=out, in_=in_, func=func, **kw), reads, writes)

        def tt_(eng, out, in0, in1, op, reads, writes):
            return R.op(eng, lambda e: e.tensor_tensor(out=out, in0=in0, in1=in1, op=op), reads, writes)

        def ts_(eng, out, in0, s1, s2, op0, op1, reads, writes):
            if s2 is None:
                return R.op(eng, lambda e: e.tensor_scalar(out=out, in0=in0, scalar1=s1, scalar2=None, op0=op0), reads, writes)
            return R.op(eng, lambda e: e.tensor_scalar(out=out, in0=in0, scalar1=s1, scalar2=s2, op0=op0, op1=op1),
                        reads, writes)

        def split_(src, bsrc, outs, bouts, tmp1, tmp2, btmp):
            cp_("dve", outs[0], src, [bsrc], [bouts])
            if len(outs) >= 2:
                tt_("dve", tmp1, src, outs[0], ALU.subtract, [bsrc, bouts], [btmp])
                cp_("dve", outs[1], tmp1, [btmp], [bouts])
            if len(outs) >= 3:
                tt_("dve", tmp2, tmp1, outs[1], ALU.subtract, [btmp, bouts], [btmp])
                cp_("dve", outs[2], tmp2, [btmp], [bouts])

        def recip_(out, in_, reads, writes):
            return R.op("dve", lambda e: e.reciprocal(out=out, in_=in_), reads, writes)

        def memset_(out, val, writes):
            return R.op("dve", lambda e: e.memset(out, val), [], writes)

        def stt_(eng, out, in0, scalar, in1, op0, op1, reads, writes):
            return R.op(eng, lambda e: e.scalar_tensor_tensor(out=out, in0=in0, scalar=scalar, in1=in1, op0=op0, op1=op1),
                        reads, writes)

        def cp_(eng, out, in_, reads, writes):
            return R.op(eng, lambda e: e.tensor_copy(out=out, in_=in_), reads, writes)

        def rmsnorm_fm(src, bsrc, n, gname, l, dst, bdst, sq, bsq, rs, brs, pbank, outf32=False):
            act(sq, src, AF.Square, [bsrc], [bsq])
            mm(bank(pbank)[:, 0:n], PB[pbank], [(onesb, sq[:, k, :]) for k in range(KC)], [bsq, B_const])
            act(rs, bank(pbank)[:, 0:n], AF.Sqrt, [PB[pbank]], [brs], bias=epst[:, 0:1], scale=1.0 / D)
            recip_(rs, rs, [brs], [brs])
            for k in range(KC):
                stt_("dve", dst[:, k, :], src[:, k, :], pcol(gname, l, k), rs, ALU.mult, ALU.mult,
                     [bsrc, brs, B_const], [bdst])

        wq_rr = [0]

        _bw2 = {nm: [Buf(f"{nm}{l_}") for l_ in range(2)] for (nm, _, _) in WSPEC}
        Bw = {nm: [_bw2[nm][l_ % 2] for l_ in range(nlayers)] for (nm, _, _) in WSPEC}
        wcur = [[]]

        def wload(slot_ap, bslot, src_rows, c0, ncols, kch, nsplit=2):
            v = src_rows.rearrange("(k p) n -> p k n", p=128)
            step = (kch + nsplit - 1) // nsplit
            for k0 in range(0, kch, step):
                k1 = min(kch, k0 + step)
                R.dma("pool", slot_ap[:, k0:k1, 0:ncols], v[:, k0:k1, c0:c0 + ncols], reads=list(wcur[0]), writes=[bslot])

        R.dma("sp", p128[:], p128_d, writes=[B_const])
        R.dma("sp", p32[:, 0:2 * DEPTH], p32_d, writes=[B_const])
        R.dma("sp", c128f[:], c128_d, writes=[B_const])
        R.dma("pool", c128b[:], c128_d, writes=[B_const])
        R.dma("sp", c32[:], c32_d, writes=[B_const])
        R.dma("sp", corec[:], core_d, writes=[B_const])
        R.dma("sp", hhalo[:], xhalo_d, writes=[B_hhalo])
        if not unit:
            Bbn = {nm: Buf(nm + "_bnc") for (nm, _, _) in WSPEC}
            for (nm, rows, cols) in WSPEC:
                R.dma("sp", wbnc[nm], wext[nm], writes=[Bbn[nm]])
            def gather_layer(l_):
                for wi, (nm, rows, cols) in enumerate(WSPEC):
                    rs_ = rows // 8
                    R.coll_sem(sems[38 + 2 * wi + (l_ % 2)], ("w", nm, l_ % 2), [wbnc[nm][l_ * rs_:(l_ + 1) * rs_, :]],
                               [wfull[nm][l_]], [list(range(8))], reads=[Bbn[nm]], writes=[Bw[nm][l_]])
            for l_ in range(min(2, nlayers)):
                gather_layer(l_)
        memset_(epst[:], EPS, [B_const])
        memset_(dtT[:], 0.0, [B_dtT])
        memset_(AT[:], 0.0, [B_AT])
        memset_(decin[:], 0.0, [B_decin])
        for k in range(KC):
            R.dma("sp", hT_v[:, k, :], xT_v[:, k, :], writes=B_hT[k])
        R.barrier()
        A.reset()
        msrc = A.f32(16 * 256).rearrange("p (k t) -> p k t", k=16)
        msq = A.bf16(16 * 256).rearrange("p (k t) -> p k t", k=16)
        mrs = A.f32(256)
        bm1, bm2, bm3 = Buf("m1"), Buf("m2"), Buf("m3")
        memn_a = A.bf16(16 * 256)
        bm4 = Buf("m4")
        R.dma("sp", msrc, memT_v, writes=[bm1])
        rmsnorm_fm(msrc, bm1, 256, "mem_norm", 0, memn_a.rearrange("p (k t) -> p k t", k=16), bm4, msq, bm2, mrs, bm3, 7)
        R.dma("sp", memn_d, memn_a, reads=[bm4], writes=[B_memn])
        R.barrier()

        for l in range(nlayers):
          try:
            if (not unit) and l >= 1 and l + 1 < nlayers:
                gather_layer(l + 1)
            if unit == "B":
                for k in range(24):
                    R.dma("sp", xbcT_v[:, k, :], fm(u_xbc)[:, k, :], writes=[B_xbc[k]])
                for k in range(16):
                    R.dma("sp", zsT_v[:, k, :], fm(u_zs)[:, k, :], writes=[B_zs[k]])
                R.dma("sp", dtT[0:32, :], u_dt, writes=[B_dtT])
                R.barrier()
            else:
             A.reset()
             h1T = A.bf16(16 * T).rearrange("p (k t) -> p k t", k=16)
             h1h = A.bf16(16 * 8).rearrange("p (k t) -> p k t", k=16)
             Bh1, Bh1h = Buf("h1T"), Buf("h1h")
             mark = A.off
             hld = [A.f32(16 * TT).rearrange("p (k t) -> p k t", k=16) for _ in range(2)]
             Bhld = [Buf("hld0"), Buf("hld1")]
             sq = A.bf16(16 * TT).rearrange("p (k t) -> p k t", k=16)
             Bsq = Buf("sq")
             rs = A.f32(TT)
             Brs = Buf("rs")
             hhv = hhalo[:].rearrange("p (k t) -> p k t", k=16)
             rmsnorm_fm(hhv, B_hhalo, 3, "norm_mix", l, h1h[:, :, 0:3], Bh1h, sq[:, :, 0:3], Bsq, rs[:, 0:3], Brs, 7)
             for tt in range(NTT):
                 b = tt % 2
                 R.dma("sp", hld[b], hT_v[:, :, tt * TT:(tt + 1) * TT], reads=[B_hT[k][tt] for k in range(KC)],
                       writes=[Bhld[b]])
                 rmsnorm_fm(hld[b], Bhld[b], TT, "norm_mix", l, h1T[:, :, tt * TT:(tt + 1) * TT], Bh1, sq, Bsq, rs, Brs, 6 + (tt % 2))
             R.barrier()
             A.reset(mark)
             wsl = [A.bf16(16 * 512).rearrange("p (k n) -> p k n", k=16) for _ in range(2)]
             Bws = [Buf("ws0"), Buf("ws1")]
             xb = [A.f32(3 + T) for _ in range(2)]
             Bxb = [Buf("xb0"), Buf("xb1")]
             acc = [A.f32(T) for _ in range(2)]
             Bacc = [Buf("acc0"), Buf("acc1")]
             ot = [A.f32(T) for _ in range(2)]
             Bot = [Buf("ot0"), Buf("ot1")]
             pcb = A.f32(T)
             Bpcb = Buf("pcb")
             sqb = A.bf16(T)
             Bsqb = Buf("sqb")
             rsb = A.f32(TT)
             Brsb = Buf("rsb")
             ysb = [A.bf16(T) for _ in range(2)]
             Bys = [Buf("ys0"), Buf("ys1")]
             wl = wfull["w_in"][l]
             wcur[0] = [Bw["w_in"][l]]

             blocks = []
             for bi in range(6):
                 blocks.append((OFF_XBC + bi * 512, 512, [("xbc", bi * 4 + j, j * 128, 128) for j in range(4)]))
             blocks.append((OFF_DT, 32, [("dt", 0, 0, 32)]))
             for bi in range(4):
                 blocks.append((OFF_Z + bi * 512, 512, [("z", bi * 4 + j, j * 128, 128) for j in range(4)]))
             for j in range(16):
                 blocks.append((None, 384, [("u", j, 0, 128), ("c", j, 128, 128), ("b", j, 256, 128)], j))
             pbr = [0]
             chunk_ct = [0]
             deferred = []

             def issue_wload(bi):
                 blk = blocks[bi]
                 s = bi % 2
                 if blk[0] is None:
                     j = blk[3]
                     for q, off in enumerate((OFF_U, OFF_C, OFF_B)):
                         wload(wsl[s][:, :, q * 128:(q + 1) * 128], Bws[s], wl, off + j * 128, 128, 16, nsplit=1)
                 else:
                     wload(wsl[s], Bws[s], wl, blk[0], blk[1], 16)

             issue_wload(0)
             for bi, blk in enumerate(blocks):
                 if bi + 1 < len(blocks):
                     issue_wload(bi + 1)
                 s = bi % 2
                 for (kind, idx, coff, wd) in blk[2]:
                     ci = chunk_ct[0]
                     chunk_ct[0] += 1
                     w = ci % 2
                     need_halo = kind in ("xbc", "u", "c")
                     tiles = ([-1] if need_halo else []) + list(range(NTT))
                     for tt in tiles:
                         pbk = pbr[0] % 6
                         pbr[0] += 1
                         if tt < 0:
                             n, rhsf, c0 = 3, (lambda k: h1h[:, k, 0:3]), 0
                             rb = [Bh1h]
                         else:
                             n, rhsf, c0 = TT, (lambda k, tt=tt: h1T[:, k, tt * TT:(tt + 1) * TT]), 3 + tt * TT
                             rb = [Bh1]
                         pso = bank(pbk)[0:wd, 0:n]
                         mm(pso, PB[pbk], [(wsl[s][:, k, coff:coff + wd], rhsf(k)) for k in range(KC)], rb + [Bws[s]])
                         if kind == "xbc":
                             act(xb[w][:, c0:c0 + n], pso, AF.Copy, [PB[pbk]], [Bxb[w]])
                         elif kind == "u":
                             act(xb[w][:, c0:c0 + n], pso, AF.Copy, [PB[pbk]], [Bxb[w]])
                         elif kind == "c":
                             wu = (ci - 1) % 2
                             tt_("dve", xb[wu][:, c0:c0 + n], pso, xb[wu][:, c0:c0 + n], ALU.mult, [PB[pbk], Bxb[wu]], [Bxb[wu]])
                         elif kind == "b":
                             tt_("dve", ot[w][:, c0 - 3:c0 - 3 + n], pso, pcb[:, c0 - 3:c0 - 3 + n], ALU.mult, [PB[pbk], Bpcb], [Bot[w]])
                         elif kind == "z":
                             act(ot[w][:, c0 - 3:c0 - 3 + n], pso, AF.Silu, [PB[pbk]], [Bot[w]])
                         elif kind == "dt":
                             act(dtT[0:32, tt * TT:(tt + 1) * TT], pso, AF.Exp, [PB[pbk], B_const], [B_dtT], bias=p32[:, l:l + 1])
                     if kind == "xbc":
                         cw = lambda tap: pcol("conv_w", l, tap * 24 + idx)
                         ts_("dve", acc[w], xb[w][:, 3:3 + T], cw(3), pcol("conv_b", l, idx), ALU.mult, ALU.add,
                             [Bxb[w], B_const], [Bacc[w]])
                         for tap in (2, 1, 0):
                             stt_("dve", acc[w], xb[w][:, tap:tap + T], cw(tap), acc[w], ALU.mult, ALU.add,
                                  [Bxb[w], Bacc[w], B_const], [Bacc[w]])
                         act(ot[w], acc[w], AF.Silu, [Bacc[w]], [Bot[w]])
                         R.dma("sp", xbcT_v[:, idx, :], ot[w], reads=[Bot[w]], writes=[B_xbc[idx]])
                     elif kind == "z":
                         R.dma("sp", zsT_v[:, idx, :], ot[w], reads=[Bot[w]], writes=[B_zs[idx]])
                     elif kind == "dt":
                         act(dtT[0:32, :], dtT[0:32, :], AF.Ln, [B_dtT], [B_dtT], bias=1.0)
                     elif kind == "c":
                         wu = (ci - 1) % 2
                         cw = lambda tap: pcol("sc_w", l, tap * 16 + idx)
                         ts_("dve", pcb, xb[wu][:, 3:3 + T], cw(2), None, ALU.mult, ALU.bypass, [Bxb[wu], B_const], [Bpcb])
                         stt_("dve", pcb, xb[wu][:, 2:2 + T], cw(1), pcb, ALU.mult, ALU.add, [Bxb[wu], Bpcb, B_const], [Bpcb])
                         stt_("dve", pcb, xb[wu][:, 1:1 + T], cw(0), pcb, ALU.mult, ALU.add, [Bxb[wu], Bpcb, B_const], [Bpcb])
                     elif kind == "b":
                         def fin(w=w, idx=idx):
                             act(sqb, ot[w], AF.Square, [Bot[w]], [Bsqb])
                             for tt in range(NTT):
                                 pbk = 6 + (tt % 2)
                                 mm(bank(pbk), PB[pbk], [(onesb, sqb[:, tt * TT:(tt + 1) * TT])], [Bsqb, B_const])
                                 act(rsb, bank(pbk), AF.Sqrt, [PB[pbk]], [Brsb], bias=epst[:, 0:1], scale=1.0 / 128)
                                 recip_(rsb, rsb, [Brsb], [Brsb])
                                 stt_("dve", ysb[w][:, tt * TT:(tt + 1) * TT], ot[w][:, tt * TT:(tt + 1) * TT],
                                      pcol("sc_norm", l, idx), rsb, ALU.mult, ALU.mult, [Bot[w], Brsb, B_const], [Bys[w]])
                             R.dma("sp", ycatT_v[:, 16 + idx, :], ysb[w], reads=[Bys[w]], writes=[B_ycat[16 + idx]])
                         deferred.append(fin)
                 if len(deferred) > 1:
                     deferred.pop(0)()
             while deferred:
                 deferred.pop(0)()
             R.barrier()
            if stop == "A":
                raise _Stop()

            R.mark("B")
            A.reset()
            sel32a = A.bf16(32 * 128)
            R.dma("pool", sel32a, sel32_d, writes=[B_const])
            sel32v = sel32a.rearrange("p (h m) -> p h m", h=32)
            tokp = A.f32(8 * 512)

            def tokv(i):
                return tokp[:, i * 512:(i + 1) * 512].rearrange("p (c h) -> p c h", c=16)
            dt_tok, A_tok, negA_tok, Rend, Rstart, w_st, dec_off, cdk = [tokv(i) for i in range(8)]
            adt = A.f32(T)
            ones_r = A.f32(T)
            Bt = Buf("adt")
            act(p32[:, 2 * DEPTH:2 * DEPTH + 1], p32[:, DEPTH + l:DEPTH + l + 1], AF.Exp, [B_const], [Bt])
            ts_("dve", adt[0:32, :], dtT[0:32, :], p32[:, 2 * DEPTH:2 * DEPTH + 1], -1.0, ALU.mult, ALU.mult, [B_dtT, Bt], [Bt])
            memset_(ones_r[0:32, :], 1.0, [Bt])
            R.op("dve", lambda e, d0=ones_r[0:32, :], d1=adt[0:32, :]: e.tensor_tensor_scan(
                out=AT[0:32, :], data0=d0, data1=d1, initial=0.0, op0=ALU.mult, op1=ALU.add), [Bt], [B_AT])
            act(decin[0:32, :], AT[0:32, :], AF.Exp, [B_AT], [B_decin])
            R.dma("sp", gD_in.rearrange("a h -> h a"), AT[0:32, T - 1:T], reads=[B_AT], writes=[B_gD_in])
            ATs = [A.bf16(T) for _ in range(3)]
            dts = [A.bf16(T) for _ in range(2)]
            BATs, Bdts, Bsp = Buf("ATs"), Buf("dts"), Buf("sptmp")
            split_(AT[:], B_AT, ATs, BATs, adt, ones_r, Bsp)
            split_(dtT[:], B_dtT, dts, Bdts, adt, ones_r, Bsp)
            ttmp = A.f32(4 * 32)
            Bttmp = Buf("ttmp")
            for c in range(NCH):
                pbk = c % 4
                pv = ps[:, pbk, :].bitcast(BF16)
                csl = slice(c * 128, (c + 1) * 128)
                srcs = [dts[0], dts[1], ATs[0], ATs[1], ATs[2]]
                for i_, sr in enumerate(srcs):
                    tr(pv[:, i_ * 128:(i_ + 1) * 128], PB[pbk], sr[:, csl], identb, [Bdts, BATs, B_const])
                slot = lambda i_: pv[:, i_ * 128:i_ * 128 + 32]
                act(ttmp[:, 0:32], slot(1), AF.Copy, [PB[pbk]], [Bttmp])
                act(ttmp[:, 32:64], slot(3), AF.Copy, [PB[pbk]], [Bttmp])
                act(ttmp[:, 64:96], slot(4), AF.Copy, [PB[pbk]], [Bttmp])
                tt_("dve", dt_tok[:, c, :], slot(0), ttmp[:, 0:32], ALU.add, [PB[pbk], Bttmp], [B_tokp])
                tt_("dve", ttmp[:, 96:128], slot(2), ttmp[:, 32:64], ALU.add, [PB[pbk], Bttmp], [Bttmp])
                tt_("dve", A_tok[:, c, :], ttmp[:, 96:128], ttmp[:, 64:96], ALU.add, [Bttmp], [B_tokp])
            ts_("dve", tokp[:, 1024:1536], tokp[:, 512:1024], -1.0, None, ALU.mult, ALU.bypass, [B_tokp], [B_tokp])
            dg = A.f32(512)
            Bdg = Buf("dg")
            aend = AT[0:32, :].rearrange("p (c t) -> p c t", t=128)[:, :, 127]
            memset_(dg, 0.0, [Bdg])
            tt_("dve", dg[0:32, :].rearrange("p (c h) -> p c h", c=16), ident32.unsqueeze(1).to_broadcast([32, 16, 32]),
                aend.unsqueeze(2).to_broadcast([32, 16, 32]), ALU.mult, [B_AT, B_const], [Bdg])
            dgs = [A.bf16(512) for _ in range(3)]
            Bdgs = Buf("dgs")
            dgt = [A.f32(512) for _ in range(2)]
            split_(dg, Bdg, dgs, Bdgs, dgt[0], dgt[1], Bsp)
            mm(bank(4), PB[4], [(onesb, dgs[0]), (onesb, dgs[1]), (onesb, dgs[2])], [Bdgs, B_const])
            cp_("dve", tokp[:, 3 * 512:4 * 512], bank(4), [PB[4]], [B_tokp])
            memset_(tokp[:, 4 * 512:4 * 512 + 32], 0.0, [B_tokp])
            cp_("dve", tokp[:, 4 * 512 + 32:5 * 512], tokp[:, 3 * 512:4 * 512 - 32], [B_tokp], [B_tokp])
            tmpk = A.f32(512)
            Btk = Buf("tmpk")
            tkv = tmpk.rearrange("p (c h) -> p c h", c=16)
            tt_("dve", tkv, Rend, A_tok, ALU.subtract, [B_tokp], [Btk])
            act(tkv, tkv, AF.Exp, [Btk], [Btk])
            tt_("dve", w_st, tkv, dt_tok, ALU.mult, [Btk, B_tokp], [B_tokp])
            tt_("dve", tkv, A_tok, Rstart, ALU.subtract, [B_tokp], [Btk])
            act(dec_off, tkv, AF.Exp, [Btk], [B_tokp])
            tt_("dve", tkv, Rend, Rstart, ALU.subtract, [B_tokp], [Btk])
            act(cdk, tkv, AF.Exp, [Btk], [B_tokp])

            R.mark("Bgroups")
            xsb = A.bf16(4 * T).rearrange("p (k t) -> p k t", k=4)
            BTb = A.bf16(T)
            CTb = A.bf16(T)
            Bxs, BBT, BCT = Buf("xsb"), Buf("BTb"), Buf("CTb")
            yTg = A.f32(4 * T).rearrange("p (k t) -> p k t", k=4)
            ByTg = Buf("yTg")
            S = A.f32(512)
            BS = Buf("S")
            prevb = A.bf16(512)
            Bprev = Buf("prevb")
            xdt = [A.bf16(512) for _ in range(2)]
            xw = [A.bf16(512) for _ in range(2)]
            Btok_ = [A.bf16(128) for _ in range(2)]
            cbT = [A.f32(128) for _ in range(2)]
            dec = [A.f32(1024) for _ in range(2)]
            scT = [A.bf16(1024) for _ in range(2)]
            tmpy = [A.f32(512) for _ in range(2)]
            ytok = [A.f32(512) for _ in range(2)]
            yhl = [[A.bf16(512) for _ in range(2)] for _ in range(2)]
            ylo = [A.f32(512) for _ in range(2)]
            Byhl = [Buf("yhl0"), Buf("yhl1")]
            Bylo = [Buf("ylo0"), Buf("ylo1")]
            Bxdt, Bxw, BBtok, BcbT, Bdec, BscT, Btmpy, Bytok = [[Buf(f"{n}{i}") for i in range(2)] for n in
                                                                 ("xdt", "xw", "Btok", "cbT", "dec", "scT", "tmpy", "ytok")]
            psb0 = ps[:, 0, :].bitcast(BF16)
            import os as _os
            _gseq = [int(x) for x in _os.environ["GSEQ"].split(",")] if _os.environ.get("GSEQ") else list(range(int(_os.environ.get("GSTART", "0")), 4))
            for g in _gseq:
                if _os.environ.get("GBAR") and g > 0:
                    R.barrier()
                for j in range(4):
                    R.dma("pool", xsb[:, j, :], xbcT_v[:, 4 * g + j, :], reads=[B_xbc[4 * g + j]], writes=[Bxs])
                R.dma("pool", BTb, xbcT_v[:, 16 + g, :], reads=[B_xbc[16 + g]], writes=[BBT])
                R.dma("pool", CTb, xbcT_v[:, 20 + g, :], reads=[B_xbc[20 + g]], writes=[BCT])
                for c in range(int(_os.environ.get("NCHX", str(NCH)))):
                    w = c % 2
                    tsl = slice(c * 128, (c + 1) * 128)
                    hs = slice(g * 8, (g + 1) * 8)
                    if c < 3 or c == NCH - 1:
                        R.mark(f"g{g}c{c}")
                    for j in range(4):
                        tr(psb0[:, j * 128:(j + 1) * 128], PB[0], xsb[:, j, tsl], identb, [Bxs, B_const])
                    tr(psb0[:, 512:640], PB[0], BTb[:, tsl], identb, [BBT, B_const])
                    x3 = psb0[:, 0:512].rearrange("p (r q) -> p r q", r=8)
                    tt_("dve", xdt[w].rearrange("p (r q) -> p r q", r=8), x3,
                        dt_tok[:, c, hs].unsqueeze(2).to_broadcast([128, 8, 64]), ALU.mult, [PB[0], B_tokp], [Bxdt[w]])
                    tt_("dve", xw[w].rearrange("p (r q) -> p r q", r=8), x3,
                        w_st[:, c, hs].unsqueeze(2).to_broadcast([128, 8, 64]), ALU.mult, [PB[0], B_tokp], [Bxw[w]])
                    cp_("dve", Btok_[w], psb0[:, 512:640], [PB[0]], [BBtok[w]])
                    mm(ps[:, 1, 0:128], PB[1], [(BTb[:, tsl], CTb[:, tsl])], [BBT, BCT])
                    act(cbT[w], ps[:, 1, 0:128], AF.Copy, [PB[1]], [BcbT[w]])
                    for r in range(8):
                        h = g * 8 + r
                        pbk = 2 + r // 4
                        o_ = ps[:, pbk, (r % 4) * 128:(r % 4 + 1) * 128]
                        mm(o_, PB[pbk], [(sel32v[:, h, :], ATs[0][:, tsl]), (sel32v[:, h, :], ATs[1][:, tsl]),
                                         (sel32v[:, h, :], ATs[2][:, tsl]), (identb, negmb)], [BATs, B_const])
                    for r in range(8):
                        h = g * 8 + r
                        pbk = 2 + r // 4
                        o_ = ps[:, pbk, (r % 4) * 128:(r % 4 + 1) * 128]
                        act(dec[w][:, r * 128:(r + 1) * 128], o_, AF.Exp, [PB[pbk], B_tokp], [Bdec[w]],
                            bias=negA_tok[:, c, h:h + 1])
                    tt_("dve", scT[w].rearrange("p (r q) -> p r q", r=8), dec[w].rearrange("p (r q) -> p r q", r=8),
                        cbT[w].unsqueeze(1).to_broadcast([128, 8, 128]), ALU.mult, [Bdec[w], BcbT[w]], [BscT[w]])
                    for r in range(8):
                        mm(ps[:, 4, r * 64:(r + 1) * 64], PB[4],
                           [(scT[w][:, r * 128:(r + 1) * 128], xdt[w][:, r * 64:(r + 1) * 64])], [BscT[w], Bxdt[w]])
                    if c > 0:
                        mm(bank(5), PB[5], [(CTb[:, tsl], prevb)], [BCT, Bprev])
                        tt_("dve", tmpy[w].rearrange("p (r q) -> p r q", r=8), bank(5).rearrange("p (r q) -> p r q", r=8),
                            dec_off[:, c, hs].unsqueeze(2).to_broadcast([128, 8, 64]), ALU.mult, [PB[5], B_tokp], [Btmpy[w]])
                        tt_("dve", ytok[w], tmpy[w], bank(4), ALU.add, [Btmpy[w], PB[4]], [Bytok[w]])
                    else:
                        act(ytok[w], bank(4), AF.Copy, [PB[4]], [Bytok[w]])
                    split_(ytok[w], Bytok[w], [yhl[w][0], yhl[w][1]], Byhl[w], tmpy[w], None, Btmpy[w])
                    pv6 = ps[:, 6, :].bitcast(BF16)
                    for j in range(4):
                        tr(pv6[:, j * 128:(j + 1) * 128], PB[6], yhl[w][0][:, j * 128:(j + 1) * 128], identb, [Byhl[w], B_const])
                        tr(pv6[:, 512 + j * 128:512 + (j + 1) * 128], PB[6], yhl[w][1][:, j * 128:(j + 1) * 128], identb, [Byhl[w], B_const])
                    act(ylo[w].rearrange("p (k t) -> p k t", k=4), pv6[:, 512:1024].rearrange("p (k t) -> p k t", k=4), AF.Copy, [PB[6]], [Bylo[w]])
                    tt_("dve", yTg[:, :, tsl], pv6[:, 0:512].rearrange("p (k t) -> p k t", k=4), ylo[w].rearrange("p (k t) -> p k t", k=4),
                        ALU.add, [PB[6], Bylo[w]], [ByTg])
                    mm(bank(7), PB[7], [(Btok_[w], xw[w])], [BBtok[w], Bxw[w]])
                    if c > 0:
                        tt_("dve", S.rearrange("p (r q) -> p r q", r=8), S.rearrange("p (r q) -> p r q", r=8),
                            cdk[:, c, hs].unsqueeze(2).to_broadcast([128, 8, 64]), ALU.mult, [BS, B_tokp], [BS])
                        tt_("dve", S, S, bank(7), ALU.add, [BS, PB[7]], [BS])
                    else:
                        cp_("dve", S, bank(7), [PB[7]], [BS])
                    if c < NCH - 1:
                        cp_("dve", prevb, S, [BS], [Bprev])
                R.mark(f"g{g}end")
                for j in range(4):
                    R.dma("sp", ylT_v[:, 4 * g + j, :], yTg[:, j, :], reads=[ByTg], writes=[B_yl[4 * g + j]])
                R.dma("sp", gS_in[0:128, g * 512:(g + 1) * 512], S, reads=[BS], writes=[B_gS_in])
            R.barrier()
            if stop == "B":
                raise _Stop()
            R.mark("X")
            R.coll([gS_in], [gS_out], GROUPS, reads=[B_gS_in], writes=[B_gS_out])
            R.coll([gD_in], [gD_out], GROUPS, reads=[B_gD_in], writes=[B_gD_out])
            R.barrier()
            if stop == "X":
                raise _Stop()

            A.reset()
            sel2a = A.bf16(16 * 128)
            R.dma("pool", sel2a, sel2_d, writes=[B_const])
            dcs = [A.bf16(T) for _ in range(2)]
            Bdcs, Bsp = Buf("dcs"), Buf("sptmp")
            sptmp = A.f32(T)
            split_(decin[:], B_decin, dcs, Bdcs, sptmp, None, Bsp)
            corecb = A.bf16(16)
            Bcorecb = Buf("corecb")
            cp_("dve", corecb, corec[:], [B_const], [Bcorecb])
            sel2v = sel2a.rearrange("p (c m) -> p c m", c=16)
            Sin = A.f32(2048)
            BSin = Buf("Sin")
            Sinb = A.bf16(2048)
            BSinb = Buf("Sinb")
            Sj = [A.f32(2048) for _ in range(2)]
            BSj = [Buf("Sj0"), Buf("Sj1")]
            ld4 = A.f32(32)
            Bld4 = Buf("ld4")
            coef = A.f32(32)
            cfd = A.f32(128)
            cbc = A.f32(128)
            Bcf = Buf("coef")
            memset_(ld4, 0.0, [Bld4])
            R.dma("sp", ld4[0:4, :], gD_out, reads=[B_gD_out], writes=[Bld4])
            lds = [A.bf16(32) for _ in range(3)]
            Blds = Buf("lds")
            ldt_ = [A.f32(32) for _ in range(2)]
            split_(ld4, Bld4, lds, Blds, ldt_[0], ldt_[1], Bsp)
            mm(ps[0:4, 0, 0:32], PB[0], [(corecb[:, 4:8], lds[0]), (corecb[:, 4:8], lds[1]), (corecb[:, 4:8], lds[2])], [Blds, Bcorecb])
            act(coef[0:4, :], ps[0:4, 0, 0:32], AF.Exp, [PB[0]], [Bcf])
            ts_("dve", coef[0:4, :], coef[0:4, :], corec[0:4, 8:9], None, ALU.mult, ALU.bypass, [Bcf, B_const], [Bcf])
            memset_(cfd, 0.0, [Bcf])
            tt_("dve", cfd[0:4, :].rearrange("p (j h) -> p j h", j=4), ident32[0:4, 0:4].unsqueeze(2).to_broadcast([4, 4, 32]),
                coef[0:4, :].unsqueeze(1).to_broadcast([4, 4, 32]), ALU.mult, [Bcf, B_const], [Bcf])
            cfs = [A.bf16(128) for _ in range(3)]
            Bcfs = Buf("cfs")
            cft = [A.f32(128) for _ in range(2)]
            split_(cfd, Bcf, cfs, Bcfs, cft[0], cft[1], Bsp)
            mm(ps[:, 1, 0:128], PB[1], [(onesb, cfs[0]), (onesb, cfs[1]), (onesb, cfs[2])], [Bcfs, B_const])
            cp_("dve", cbc, ps[:, 1, 0:128], [PB[1]], [Bcf])
            cbv = cbc.rearrange("p (j h) -> p j h", j=4)
            for j in range(4):
                R.dma("sp", Sj[j % 2], gS_out[j * 128:(j + 1) * 128, :], reads=[B_gS_out], writes=[BSj[j % 2]])
                sj3 = Sj[j % 2].rearrange("p (h q) -> p h q", h=32)
                tt_("dve", sj3, sj3, cbv[:, j, :].unsqueeze(2).to_broadcast([128, 32, 64]), ALU.mult, [BSj[j % 2], Bcf], [BSj[j % 2]])
                if j == 0:
                    cp_("dve", Sin, Sj[0], [BSj[0]], [BSin])
                else:
                    tt_("dve", Sin, Sin, Sj[j % 2], ALU.add, [BSin, BSj[j % 2]], [BSin])
            cp_("dve", Sinb, Sin, [BSin], [BSinb])
            CT4 = A.bf16(4 * T).rearrange("p (k t) -> p k t", k=4)
            BCT4 = Buf("CT4")
            for g in range(4):
                R.dma("pool", CT4[:, g, :], xbcT_v[:, 20 + g, :], reads=[B_xbc[20 + g]], writes=[BCT4])
            ygg = A.f32(4 * T).rearrange("p (k t) -> p k t", k=4)
            Bygg = Buf("ygg")
            ldt = [[A.f32(TT) for _ in range(3)] for _ in range(2)]
            Bldt = [[Buf(f"ldt{i}{q}") for q in range(3)] for i in range(2)]
            dsb = [A.f32(TT) for _ in range(2)]
            Bdsb = [Buf("dsb0"), Buf("dsb1")]
            t1 = [A.f32(TT) for _ in range(2)]
            Bt1 = [Buf("t10"), Buf("t11")]
            sqc = [A.bf16(TT) for _ in range(2)]
            Bsqc = [Buf("sqc0"), Buf("sqc1")]
            rsc = A.f32(TT)
            Brsc = Buf("rsc")
            ynb = [A.bf16(T) for _ in range(2)]
            Bynb = [Buf("ynb0"), Buf("ynb1")]
            it = 0
            for g in range(4):
                for mi in range(4):
                    m = 4 * g + mi
                    for tt in range(NTT):
                        w = it % 2
                        it += 1
                        tsl = slice(tt * TT, (tt + 1) * TT)
                        R.dma("sp", ldt[w][0], ylT_v[:, m, tsl], reads=[B_yl[m]], writes=[Bldt[w][0]])
                        R.dma("sp", ldt[w][1], xbcT_v[:, m, tsl], reads=[B_xbc[m]], writes=[Bldt[w][1]])
                        R.dma("sp", ldt[w][2], zsT_v[:, m, tsl], reads=[B_zs[m]], writes=[Bldt[w][2]])
                        pa, pb_ = (0, 1) if w == 0 else (2, 3)
                        mm(bank(pa), PB[pa], [(Sinb[:, m * 128:(m + 1) * 128], CT4[:, g, tsl])], [BSinb, BCT4])
                        mm(bank(pb_), PB[pb_], [(sel2v[:, m, :], dcs[0][:, tsl]), (sel2v[:, m, :], dcs[1][:, tsl])], [Bdcs, B_const])
                        act(dsb[w], bank(pb_), AF.Copy, [PB[pb_]], [Bdsb[w]])
                        tt_("dve", t1[w], bank(pa), dsb[w], ALU.mult, [PB[pa], Bdsb[w]], [Bt1[w]])
                        tt_("dve", t1[w], t1[w], ldt[w][0], ALU.add, [Bt1[w], Bldt[w][0]], [Bt1[w]])
                        stt_("dve", t1[w], ldt[w][1], pcol("dfeat", l, m), t1[w], ALU.mult, ALU.add,
                             [Bt1[w], Bldt[w][1], B_const], [Bt1[w]])
                        tt_("dve", ygg[:, mi, tsl], t1[w], ldt[w][2], ALU.mult, [Bt1[w], Bldt[w][2]], [Bygg])
                        act(sqc[w], ygg[:, mi, tsl], AF.Square, [Bygg], [Bsqc[w]])
                        mm(bank(4 + tt), PB[4 + tt], [(onesb, sqc[w])], [Bsqc[w], B_const], start=(mi == 0), stop=(mi == 3))
                for tt in range(NTT):
                    tsl = slice(tt * TT, (tt + 1) * TT)
                    act(rsc, bank(4 + tt), AF.Sqrt, [PB[4 + tt]], [Brsc], bias=epst[:, 0:1], scale=1.0 / 512)
                    recip_(rsc, rsc, [Brsc], [Brsc])
                    for mi in range(4):
                        stt_("dve", ygg[:, mi, tsl], ygg[:, mi, tsl], pcol("ssd_norm", l, 4 * g + mi), rsc,
                             ALU.mult, ALU.mult, [Bygg, Brsc, B_const], [Bygg])
                for mi in range(4):
                    cp_("dve", ynb[mi % 2], ygg[:, mi, :], [Bygg], [Bynb[mi % 2]])
                    R.dma("sp", ycatT_v[:, 4 * g + mi, :], ynb[mi % 2], reads=[Bynb[mi % 2]], writes=[B_ycat[4 * g + mi]])
            R.barrier()
            if unit == "B":
                for k in range(16):
                    R.dma("sp", fm(u_yl)[:, k, :], ylT_v[:, k, :], reads=[B_yl[k]], writes=[B_out])
                    R.dma("sp", fm(u_ycat)[:, k, :], ycatT_v[:, k, :], reads=[B_ycat[k]], writes=[B_out])
                R.barrier()
            if stop == "C":
                raise _Stop()

            def linear_residual(src_v, bsrc, kch, wrows, th, ntl, slots, Bslots, ncb, cbw, htile, Bht):
                it2 = 0
                wload(slots[0], Bslots[0], wrows, 0, cbw, kch)
                for cb in range(ncb):
                    if cb + 1 < ncb:
                        wload(slots[(cb + 1) % 2], Bslots[(cb + 1) % 2], wrows, (cb + 1) * cbw, cbw, kch)
                    s = cb % 2
                    for dj in range(cbw // 128):
                        dm = cb * (cbw // 128) + dj
                        for t_ in range(ntl):
                            tt = th + t_
                            w = it2 % 2
                            pbk = it2 % 6
                            it2 += 1
                            tsl = slice(tt * TT, (tt + 1) * TT)
                            R.dma("sp", htile[w], hT_v[:, dm, tsl], reads=[B_hT[dm][tt]], writes=[Bht[w]])
                            mm(bank(pbk), PB[pbk], [(slots[s][:, k, dj * 128:(dj + 1) * 128], src_v[:, k, t_ * TT:(t_ + 1) * TT])
                                                    for k in range(kch)], [bsrc, Bslots[s]])
                            tt_("dve", htile[w], htile[w], bank(pbk), ALU.add, [Bht[w], PB[pbk]], [Bht[w]])
                            R.dma("sp", hT_v[:, dm, tsl], htile[w], reads=[Bht[w]], writes=[B_hT[dm][tt]])

            A.reset()
            yc = A.bf16(32 * 1024).rearrange("p (k t) -> p k t", k=32)
            Byc = Buf("yc")
            wsl2 = [A.bf16(32 * 512).rearrange("p (k n) -> p k n", k=32) for _ in range(2)]
            Bws2 = [Buf("w2s0"), Buf("w2s1")]
            htile = [A.f32(TT) for _ in range(2)]
            Bht = [Buf("ht0"), Buf("ht1")]
            for half in range(2):
                for k0 in range(0, 32, 8):
                    R.dma("sp", yc[:, k0:k0 + 8, :], ycatT_v[:, k0:k0 + 8, half * 1024:(half + 1) * 1024],
                          reads=[B_ycat[k] for k in range(k0, k0 + 8)], writes=[Byc])
                wcur[0] = [Bw["w_out"][l]]
                linear_residual(yc, Byc, 32, wfull["w_out"][l], half * 2, 2, wsl2, Bws2, 4, 512, htile, Bht)
            R.barrier()
            if dbg and l == 0:
                for k in range(KC):
                    R.dma("sp", fm(dbg_d[0:D, :])[:, k, :], hT_v[:, k, :], reads=B_hT[k], writes=[B_out])
                R.barrier()

            if stop == "D":
                raise _Stop()
            A.reset()
            h2T = A.bf16(16 * T).rearrange("p (k t) -> p k t", k=16)
            Bh2 = Buf("h2T")
            mark = A.off
            hld = [A.f32(16 * TT).rearrange("p (k t) -> p k t", k=16) for _ in range(2)]
            Bhld = [Buf("hld0"), Buf("hld1")]
            sq = A.bf16(16 * TT).rearrange("p (k t) -> p k t", k=16)
            Bsq = Buf("sq")
            rs = A.f32(TT)
            Brs = Buf("rs")
            for tt in range(NTT):
                b = tt % 2
                R.dma("sp", hld[b], hT_v[:, :, tt * TT:(tt + 1) * TT], reads=[B_hT[k][tt] for k in range(KC)], writes=[Bhld[b]])
                rmsnorm_fm(hld[b], Bhld[b], TT, "norm_xa", l, h2T[:, :, tt * TT:(tt + 1) * TT], Bh2, sq, Bsq, rs, Brs, 6 + (tt % 2))
            R.barrier()
            A.reset(mark)
            wk_s = A.bf16(16 * 512).rearrange("p (k n) -> p k n", k=16)
            wv_s = A.bf16(16 * 512).rearrange("p (k n) -> p k n", k=16)
            wq_s = A.bf16(16 * 512).rearrange("p (k n) -> p k n", k=16)
            wo_s = [A.bf16(4 * 512).rearrange("p (k n) -> p k n", k=4) for _ in range(2)]
            Bwk, Bwv, Bwq = Buf("wk"), Buf("wv"), Buf("wq")
            Bwo = [Buf("wo0"), Buf("wo1")]
            memn_a = A.bf16(16 * 256)
            memn_v = memn_a.rearrange("p (k t) -> p k t", k=16)
            Bmem_a = Buf("memn_a")
            R.dma("sp", memn_a, memn_d, reads=[B_memn], writes=[Bmem_a])
            wcur[0] = [Bw["w_k"][l]]
            wload(wk_s, Bwk, wfull["w_k"][l], 0, 512, 16)
            wcur[0] = [Bw["w_v"][l]]
            wload(wv_s, Bwv, wfull["w_v"][l], 0, 512, 16)
            wcur[0] = [Bw["w_q"][l]]
            wload(wq_s, Bwq, wfull["w_q"][l], 0, 512, 16)
            kTb = A.bf16(4 * 256).rearrange("p (h m) -> p h m", h=4)
            vb = A.bf16(2 * 512).rearrange("p (a n) -> p a n", a=2)
            BkT, Bvb = Buf("kTb"), Buf("vb")
            qTb = A.bf16(4 * T).rearrange("p (h t) -> p h t", h=4)
            BqT = Buf("qTb")
            attnT = A.bf16(4 * T).rearrange("p (h t) -> p h t", h=4)
            Battn = Buf("attnT")
            pT = [[A.bf16(TT) for _ in range(2)] for _ in range(2)]
            BpT = [[Buf(f"pT{i}{q}") for q in range(2)] for i in range(2)]
            rden = [A.f32(TT) for _ in range(2)]
            Brden = [Buf("rden0"), Buf("rden1")]
            htile = [A.f32(TT) for _ in range(2)]
            Bht = [Buf("ht0"), Buf("ht1")]
            for h in range(4):
                mm(ps[:, h % 2, 0:256], PB[h % 2], [(wk_s[:, k, h * 128:(h + 1) * 128], memn_v[:, k, :]) for k in range(KC)], [Bwk, Bmem_a])
                act(kTb[:, h, :], ps[:, h % 2, 0:256], AF.Copy, [PB[h % 2]], [BkT])
            for a in range(2):
                mm(bank(2 + a), PB[2 + a], [(memn_v[:, k, a * 128:(a + 1) * 128], wv_s[:, k, :]) for k in range(KC)], [Bwv, Bmem_a])
                act(vb[:, a, :], bank(2 + a), AF.Copy, [PB[2 + a]], [Bvb])
            it = 0
            for h in range(4):
                for tt in range(NTT):
                    pbk = 4 + it % 4
                    it += 1
                    mm(bank(pbk), PB[pbk], [(wq_s[:, k, h * 128:(h + 1) * 128], h2T[:, k, tt * TT:(tt + 1) * TT]) for k in range(KC)], [Bwq, Bh2])
                    act(qTb[:, h, tt * TT:(tt + 1) * TT], bank(pbk), AF.Copy, [PB[pbk]], [BqT])
            it = 0
            for h in range(4):
                for tt in range(NTT):
                    w = it % 2
                    it += 1
                    tsl = slice(tt * TT, (tt + 1) * TT)
                    b0 = 4 * w
                    for a in range(2):
                        mm(bank(b0 + a), PB[b0 + a], [(kTb[:, h, a * 128:(a + 1) * 128], qTb[:, h, tsl])], [BkT, BqT])
                        act(pT[w][a], bank(b0 + a), AF.Exp, [PB[b0 + a]], [BpT[w][a]], scale=128.0 ** -0.5)
                    mm(bank(b0 + 2), PB[b0 + 2], [(onesb, pT[w][0]), (onesb, pT[w][1])], [BpT[w][0], BpT[w][1], B_const])
                    recip_(rden[w], bank(b0 + 2), [PB[b0 + 2]], [Brden[w]])
                    mm(bank(b0 + 3), PB[b0 + 3], [(vb[:, a, h * 128:(h + 1) * 128], pT[w][a]) for a in range(2)], [Bvb, BpT[w][0], BpT[w][1]])
                    tt_("dve", attnT[:, h, tsl], bank(b0 + 3), rden[w], ALU.mult, [PB[b0 + 3], Brden[w]], [Battn])
            wcur[0] = [Bw["w_o"][l]]
            linear_residual(attnT, Battn, 4, wfull["w_o"][l], 0, 4, wo_s, Bwo, 4, 512, htile, Bht)
            R.barrier()
            if dbg and l == 0:
                for k in range(KC):
                    R.dma("sp", fm(dbg_d[D:2 * D, :])[:, k, :], hT_v[:, k, :], reads=B_hT[k], writes=[B_out])
                R.barrier()

            if stop == "E":
                raise _Stop()
            for half in range(2):
                A.reset()
                h3 = A.bf16(16 * 1024).rearrange("p (k t) -> p k t", k=16)
                Bh3 = Buf("h3")
                actT = A.bf16(44 * 1024).rearrange("p (k t) -> p k t", k=44)
                BactT = Buf("actT")
                mark2 = A.off
                hld = [A.f32(16 * TT).rearrange("p (k t) -> p k t", k=16) for _ in range(1)]
                Bhld = [Buf("hld0")]
                sq = A.bf16(16 * TT).rearrange("p (k t) -> p k t", k=16)
                Bsq = Buf("sq")
                rs = A.f32(TT)
                Brs = Buf("rs")
                for t_ in range(2):
                    tt = half * 2 + t_
                    R.dma("sp", hld[0], hT_v[:, :, tt * TT:(tt + 1) * TT], reads=[B_hT[k][tt] for k in range(KC)], writes=[Bhld[0]])
                    rmsnorm_fm(hld[0], Bhld[0], TT, "norm_ffn", l, h3[:, :, t_ * TT:(t_ + 1) * TT], Bh3, sq, Bsq, rs, Brs, 6 + t_)
                R.barrier()
                A.reset(mark2)
                gus = [A.bf16(2 * 16 * 256).rearrange("p (a k n) -> p a k n", a=2, k=16) for _ in range(2)]
                Bgus = [Buf("gus0"), Buf("gus1")]
                sg = [A.f32(TT) for _ in range(2)]
                Bsg = [Buf("sg0"), Buf("sg1")]
                htile = [A.f32(TT) for _ in range(2)]
                Bht = [Buf("ht0"), Buf("ht1")]
                wg = wfull["w_gate"][l]
                wu_ = wfull["w_up"][l]

                def gul(fb):
                    wcur[0] = [Bw["w_gate"][l]]
                    wload(gus[fb % 2][:, 0], Bgus[fb % 2], wg, fb * 256, 256, 16)
                    wcur[0] = [Bw["w_up"][l]]
                    wload(gus[fb % 2][:, 1], Bgus[fb % 2], wu_, fb * 256, 256, 16)
                gul(0)
                it = 0
                for fb in range(22):
                    if fb + 1 < 22:
                        gul(fb + 1)
                    s = fb % 2
                    for fc in range(2):
                        f = fb * 2 + fc
                        for t_ in range(2):
                            w = it % 2
                            pg, pu = (0, 1) if it % 3 == 0 else ((2, 3) if it % 3 == 1 else (4, 5))
                            it += 1
                            rhs = lambda k: h3[:, k, t_ * TT:(t_ + 1) * TT]
                            mm(bank(pg), PB[pg], [(gus[s][:, 0, k, fc * 128:(fc + 1) * 128], rhs(k)) for k in range(KC)], [Bgus[s], Bh3])
                            mm(bank(pu), PB[pu], [(gus[s][:, 1, k, fc * 128:(fc + 1) * 128], rhs(k)) for k in range(KC)], [Bgus[s], Bh3])
                            act(sg[w], bank(pg), AF.Silu, [PB[pg]], [Bsg[w]])
                            tt_("dve", actT[:, f, t_ * TT:(t_ + 1) * TT], sg[w], bank(pu), ALU.mult, [Bsg[w], PB[pu]], [BactT])
                R.barrier()
                A.reset(mark2)
                dws = [A.bf16(44 * 256).rearrange("p (k n) -> p k n", k=44) for _ in range(2)]
                Bdws = [Buf("dws0"), Buf("dws1")]
                htile = [A.f32(TT) for _ in range(2)]
                Bht = [Buf("ht0"), Buf("ht1")]
                wcur[0] = [Bw["w_down"][l]]
                linear_residual(actT, BactT, 44, wfull["w_down"][l], half * 2, 2, dws, Bdws, 8, 256, htile, Bht)
                R.barrier()
            if dbg and l == 0:
                for k in range(KC):
                    R.dma("sp", fm(dbg_d[2 * D:3 * D, :])[:, k, :], hT_v[:, k, :], reads=B_hT[k], writes=[B_out])
                R.barrier()

            if l + 1 < nlayers:
                R.dma("sp", gH_in.rearrange("p (k t) -> p k t", k=16), hT_v[:, :, T - 3:T],
                      reads=[B_hT[k][3] for k in range(KC)], writes=[B_gH_in])
                R.barrier()
                R.coll([gH_in], [gH_out], GROUPS, reads=[B_gH_in], writes=[B_gH_out])
                R.barrier()
                A.reset()
                hg = [A.f32(48) for _ in range(4)]
                Bhg = Buf("hg")
                for j in range(4):
                    R.dma("sp", hg[j], gH_out[j * 128:(j + 1) * 128, :], reads=[B_gH_out], writes=[Bhg])
                ts_("dve", hhalo[:], hg[0], corec[:, 0:1], None, ALU.mult, ALU.bypass, [Bhg, B_const], [B_hhalo])
                for j in range(1, 4):
                    stt_("dve", hhalo[:], hg[j], corec[:, j:j + 1], hhalo[:], ALU.mult, ALU.add, [Bhg, B_const, B_hhalo], [B_hhalo])
                R.barrier()

          except _Stop:
            break
        A.reset()
        hld = [A.f32(16 * TT).rearrange("p (k t) -> p k t", k=16) for _ in range(2)]
        Bhld = [Buf("hld0"), Buf("hld1")]
        ofl = [A.f32(16 * TT).rearrange("p (k t) -> p k t", k=16) for _ in range(2)]
        Bofl = [Buf("of0"), Buf("of1")]
        sq = A.bf16(16 * TT).rearrange("p (k t) -> p k t", k=16)
        Bsq = Buf("sq")
        rs = A.f32(TT)
        Brs = Buf("rs")
        for tt in range(NTT):
            b = tt % 2
            R.dma("sp", hld[b], hT_v[:, :, tt * TT:(tt + 1) * TT], reads=[B_hT[k][tt] for k in range(KC)], writes=[Bhld[b]])
            rmsnorm_fm(hld[b], Bhld[b], TT, "norm_final", 0, ofl[b], Bofl[b], sq, Bsq, rs, Brs, 6 + (tt % 2))
            R.dma("sp", outT_v[:, :, tt * TT:(tt + 1) * TT], ofl[b], reads=[Bofl[b]], writes=[B_out])
        R.barrier()
        R.emit(block)
    return nc


def _consts():
    ident = np.eye(128, dtype=np.float32)
    ones = np.ones((128, 128), np.float32)
    s = np.arange(128)[:, None]
    lq = np.arange(128)[None, :]
    negm = np.where(lq < s, -30000.0, 0.0).astype(np.float32)
    c128 = np.concatenate([ident, ones, negm], 1)
    sel32 = np.zeros((128, 32, 128), np.float32)
    for h in range(32):
        sel32[h, h, :] = 1.0
    sel2 = np.zeros((128, 16, 128), np.float32)
    for m in range(16):
        sel2[2 * m, m, 0:64] = 1.0
        sel2[2 * m + 1, m, 64:128] = 1.0
    c32 = np.concatenate([np.eye(32, dtype=np.float32), np.ones((32, 128), np.float32)], 1)
    return c128, sel32.reshape(128, -1), sel2.reshape(128, -1), c32


def _corec(rank):
    c = np.zeros((128, 16), np.float32)
    if rank > 0:
        c[:, rank - 1] = 1.0
    for i in range(4):
        for j in range(4):
            if j < i < rank:
                c[i, 4 + j] = 1.0
    for j in range(4):
        if j < rank:
            c[j, 8] = 1.0
    return c


def _ptab(inp):
    cols = []

    def fmcols(a):
        L = a.shape[0]
        n = a.shape[1] // 128
        return a.reshape(L, n, 128).transpose(2, 0, 1).reshape(128, L * n)
    cols.append(fmcols(inp["norm_mix"]))
    cols.append(fmcols(inp["norm_xa"]))
    cols.append(fmcols(inp["norm_ffn"]))
    cw = inp["ssd_conv_w"]
    cols.append(cw.reshape(DEPTH, 4, 24, 128).transpose(3, 0, 1, 2).reshape(128, DEPTH * 96))
    cols.append(fmcols(inp["ssd_conv_b"]))
    sw = inp["sc_conv_w"]
    cols.append(sw.reshape(DEPTH, 3, 16, 128).transpose(3, 0, 1, 2).reshape(128, DEPTH * 48))
    cols.append(fmcols(inp["ssd_norm"]))
    cols.append(fmcols(inp["sc_norm"]))
    cols.append(fmcols(np.repeat(inp["d_skip"], 64, axis=1)))
    cols.append(fmcols(inp["norm_final"][None, :]))
    cols.append(fmcols(inp["mem_norm"][None, :]))
    p128 = np.ascontiguousarray(np.concatenate(cols, 1).astype(np.float32))
    p32 = np.ascontiguousarray(np.concatenate([inp["dt_bias"].T, inp["a_log"].T], 1).astype(np.float32))
    return p128, p32


def make_in_maps(inp, nl=DEPTH, stop=None):
    x, mem = inp["x"], inp["mem"]
    c128, sel32, sel2, c32 = _consts()
    p128, p32 = _ptab(inp)
    shared = {
        "w_in": inp["w_in"][:nl].reshape(nl * D, D_IN), "w_out": inp["w_out"][:nl].reshape(nl * 2 * D, D),
        "w_q": inp["w_q"][:nl].reshape(nl * D, 512), "w_k": inp["w_k"][:nl].reshape(nl * D, 512),
        "w_v": inp["w_v"][:nl].reshape(nl * D, 512), "w_o": inp["w_o"][:nl].reshape(nl * 512, D),
        "w_gate": inp["w_gate"][:nl].reshape(nl * D, D_FF), "w_up": inp["w_up"][:nl].reshape(nl * D, D_FF),
        "w_down": inp["w_down"][:nl].reshape(nl * D_FF, D),
        "p128": p128, "p32": p32, "c128": c128, "sel32": sel32, "sel2": sel2, "c32": c32,
    }
    wnames = [k for k in shared if k.startswith("w_")]
    wfull_h = {k: shared.pop(k) for k in wnames}
    maps = []
    for core in range(8):
        b, r = core // 4, core % 4
        s0 = r * T
        xT = np.ascontiguousarray(x[b, s0:s0 + T, :].T)
        if r == 0:
            hal = np.zeros((3, D), np.float32)
        else:
            hal = x[b, s0 - 3:s0, :]
        xh = np.ascontiguousarray(hal.T.reshape(16, 128, 3).transpose(1, 0, 2).reshape(128, 48))
        m = dict(shared)
        for k in wnames:
            a = wfull_h[k]
            rows = a.shape[0] // nl
            rs_ = rows // 8
            m[k] = np.ascontiguousarray(a.reshape(nl, 8, rs_, a.shape[1])[:, core].reshape(nl * rs_, a.shape[1]))
        m.update({"xT": xT, "xhalo": xh, "memT": np.ascontiguousarray(mem[b].T), "corec": _corec(r)})
        maps.append(m)
    return maps


_NC_CACHE = {}


def kernel(**inputs):
    inp = {k: np.asarray(v, dtype=np.float32) for k, v in inputs.items()}
    if "full" not in _NC_CACHE:
        _NC_CACHE["full"] = build_program(DEPTH, False)
    nc = _NC_CACHE["full"]
    maps = make_in_maps(inp)
    res = run_bass_kernel_spmd(nc, maps, core_ids=list(range(8)))
    out = np.empty((2, 8192, D), np.float32)
    for core in range(8):
        b, r = core // 4, core % 4
        out[b, r * T:(r + 1) * T, :] = res.results[core]["outT"].T
    return out
```
